# Optimizing a Trainium2 kernel written in Bass

```python
import math
import jax, jax.numpy as jnp
from jax import lax
import numpy as np

D_MODEL = 2048
BATCH = 2
SEQ = 8192
DEPTH = 4

N_MIXERS = 2
N_RWKV = (DEPTH + 1) // 2
N_DIFF = DEPTH // 2
RWKV_HEAD = 64
RWKV_HEADS = D_MODEL // RWKV_HEAD
LORA_DECAY = 96
LORA_AAA = 96
LORA_MV = 64
LORA_GATE = 128
GN_EPS = 64e-5
DIFF_QK = 128
DIFF_V = 256
DIFF_HEADS = D_MODEL // DIFF_V
Q_BLOCK = 128
REL_BUCKETS = 32
REL_MAX_EXACT = 16
REL_MAX_DIST = 128
D_FF = ((8 * D_MODEL // 3 + 127) // 128) * 128
CONV_W = 3
LN_EPS = 1e-5
ALPHA = (2 * DEPTH) ** 0.25
BETA = (8 * DEPTH) ** -0.25

kernel_name = "hybrid_rwkv7_diffattn_convglu_deepnorm"


def layer_norm(x, g, b):
    xf = x.astype(jnp.float32)
    mu = xf.mean(-1, keepdims=True)
    var = jnp.square(xf - mu).mean(-1, keepdims=True)
    return ((xf - mu) * lax.rsqrt(var + LN_EPS) * g + b).astype(x.dtype)


def token_shift(x):
    return jnp.pad(x, ((0, 0), (1, 0), (0, 0)))[:, :-1]


def wkv7_scan(r, decay, k, v, a_vec, b_vec):
    B, T, H, N = r.shape

    def step(S, inp):
        r_t, w_t, k_t, v_t, a_t, b_t = inp
        sa = jnp.einsum('bhij,bhj->bhi', S, a_t)
        S = S * w_t[:, :, None, :] + sa[..., None] * b_t[:, :, None, :] + v_t[..., None] * k_t[:, :, None, :]
        y = jnp.einsum('bhij,bhj->bhi', S, r_t)
        return S, y

    xs = tuple(jnp.moveaxis(t.astype(jnp.float32), 1, 0) for t in (r, decay, k, v, a_vec, b_vec))
    S0 = jnp.zeros((B, H, N, N), jnp.float32)
    _, y = lax.scan(step, S0, xs)
    return jnp.moveaxis(y, 0, 1)


def rwkv7_mix(x, v_first, vres, mu, w_rkv, w0, w1, w2, a0, a1, a2, g1, g2, k_k, k_a, r_k, lnx_g, lnx_b, w_o):
    B, T, D = x.shape
    H, N = RWKV_HEADS, RWKV_HEAD
    xx = token_shift(x) - x
    xr, xw, xk, xv, xa, xg = (x + xx * mu[i] for i in range(6))
    r = xr @ w_rkv[0]
    k = xk @ w_rkv[1]
    v = xv @ w_rkv[2]
    w = -jax.nn.softplus(-(w0 + jnp.tanh(xw @ w1) @ w2)) - 0.5
    if vres is None:
        v_first = v
    else:
        v0, v1, v2 = vres
        v = v + (v_first - v) * jax.nn.sigmoid(v0 + (xv @ v1) @ v2)
    a = jax.nn.sigmoid(a0 + (xa @ a1) @ a2)
    g = jax.nn.sigmoid(xg @ g1) @ g2
    kk = (k * k_k).reshape(B, T, H, N).astype(jnp.float32)
    kk = kk / jnp.maximum(jnp.sqrt(jnp.sum(kk * kk, -1, keepdims=True)), 1e-12)
    k = k * (1 + (a - 1) * k_a)
    r_h = r.reshape(B, T, H, N)
    k_h = k.reshape(B, T, H, N)
    v_h = v.reshape(B, T, H, N)
    a_h = a.reshape(B, T, H, N).astype(jnp.float32)
    decay = jnp.exp(-jnp.exp(w.astype(jnp.float32))).reshape(B, T, H, N)
    y = wkv7_scan(r_h, decay, k_h, v_h, -kk, kk * a_h)
    mean = y.mean(-1, keepdims=True)
    var = jnp.square(y - mean).mean(-1, keepdims=True)
    y = ((y - mean) * lax.rsqrt(var + GN_EPS)).reshape(B, T, D) * lnx_g + lnx_b
    bonus = jnp.sum(r_h * k_h * r_k, -1, keepdims=True) * v_h
    y = (y + bonus.reshape(B, T, D)).astype(x.dtype)
    return (y * g) @ w_o, v_first


def rel_bucket(dist):
    n = jnp.maximum(dist, 0)
    large = REL_MAX_EXACT + (jnp.log(jnp.maximum(n, 1).astype(jnp.float32) / REL_MAX_EXACT)
                             / math.log(REL_MAX_DIST / REL_MAX_EXACT)
                             * (REL_BUCKETS - REL_MAX_EXACT)).astype(jnp.int32)
    large = jnp.minimum(large, REL_BUCKETS - 1)
    return jnp.where(n < REL_MAX_EXACT, n, large)


def diff_attn_mix(x, w_qkv, lam, subln_g, w_o, rel_bias, lambda_init):
    B, T, D = x.shape
    H = DIFF_HEADS
    q, k, v = jnp.split(x @ w_qkv, 3, axis=-1)
    q = q.reshape(B, T, H, 2, DIFF_QK) * (DIFF_QK ** -0.5)
    k = k.reshape(B, T, H, 2, DIFF_QK)
    v = v.reshape(B, T, H, DIFF_V)
    lamf = lam.astype(jnp.float32)
    lam_full = jnp.exp(jnp.sum(lamf[0] * lamf[1])) - jnp.exp(jnp.sum(lamf[2] * lamf[3])) + lambda_init
    k_pos = jnp.arange(T)
    n_blk = T // Q_BLOCK

    def block(i):
        start = i * Q_BLOCK
        q_blk = lax.dynamic_slice_in_dim(q, start, Q_BLOCK, axis=1)
        s = jnp.einsum('bqhcd,bkhcd->bhcqk', q_blk, k).astype(jnp.float32)
        dist = (start + jnp.arange(Q_BLOCK))[:, None] - k_pos[None, :]
        bias = jnp.transpose(rel_bias[rel_bucket(dist)], (2, 0, 1))
        s = s + bias[None, :, None].astype(jnp.float32)
        s = jnp.where(dist >= 0, s, -1e30)
        p = jax.nn.softmax(s, axis=-1)
        attn = p[:, :, 0] - lam_full * p[:, :, 1]
        return jnp.einsum('bhqk,bkhd->bqhd', attn.astype(v.dtype), v)

    o = lax.map(block, jnp.arange(n_blk))
    o = jnp.moveaxis(o, 0, 1).reshape(B, T, H, DIFF_V).astype(jnp.float32)
    o = o * lax.rsqrt(jnp.mean(o * o, -1, keepdims=True) + LN_EPS) * subln_g * (1.0 - lambda_init)
    return o.reshape(B, T, D).astype(x.dtype) @ w_o


def conv_glu(x, w_up, conv_w, conv_b, w_down):
    T = x.shape[1]
    u, g = jnp.split(x @ w_up, 2, axis=-1)
    gp = jnp.pad(g, ((0, 0), (CONV_W - 1, 0), (0, 0)))
    gc = conv_b
    for j in range(CONV_W):
        gc = gc + gp[:, j:j + T] * conv_w[j]
    return (u * jax.nn.gelu(gc)) @ w_down


def setup_inputs(seed: int = 0) -> dict:
    key = jax.random.key(seed)
    ks = iter(jax.random.split(key, 40))
    nrm = lambda shape, s: jax.random.normal(next(ks), shape, jnp.float32) * s
    D, H, N, F = D_MODEL, RWKV_HEADS, RWKV_HEAD, D_FF
    nv = max(N_RWKV - 1, 1)
    ramp = (jnp.arange(D, dtype=jnp.float32) / (D - 1)) ** 0.85
    return {
        "x": nrm((BATCH, SEQ, D), 1.0),
        "ln_g": 1.0 + nrm((DEPTH, 2, D), 0.05),
        "ln_b": nrm((DEPTH, 2, D), 0.02),
        "rw_mu": jax.random.uniform(next(ks), (N_RWKV, 6, D), jnp.float32),
        "rw_w_rkv": nrm((N_RWKV, 3, D, D), D ** -0.5),
        "rw_w0": -6.0 + 5.0 * ramp[None] + nrm((N_RWKV, D), 0.1),
        "rw_w1": nrm((N_RWKV, D, LORA_DECAY), D ** -0.5),
        "rw_w2": nrm((N_RWKV, LORA_DECAY, D), 0.1 * LORA_DECAY ** -0.5),
        "rw_a0": nrm((N_RWKV, D), 0.1),
        "rw_a1": nrm((N_RWKV, D, LORA_AAA), D ** -0.5),
        "rw_a2": nrm((N_RWKV, LORA_AAA, D), 0.3 * LORA_AAA ** -0.5),
        "rw_v0": nrm((nv, D), 0.1),
        "rw_v1": nrm((nv, D, LORA_MV), D ** -0.5),
        "rw_v2": nrm((nv, LORA_MV, D), 0.3 * LORA_MV ** -0.5),
        "rw_g1": nrm((N_RWKV, D, LORA_GATE), D ** -0.5),
        "rw_g2": nrm((N_RWKV, LORA_GATE, D), LORA_GATE ** -0.5),
        "rw_k_k": 0.85 + nrm((N_RWKV, D), 0.05),
        "rw_k_a": 1.0 + nrm((N_RWKV, D), 0.05),
        "rw_r_k": nrm((N_RWKV, H, N), 0.1),
        "rw_lnx_g": 1.0 + nrm((N_RWKV, D), 0.05),
        "rw_lnx_b": nrm((N_RWKV, D), 0.02),
        "rw_w_o": nrm((N_RWKV, D, D), BETA * D ** -0.5),
        "da_w_qkv": nrm((N_DIFF, D, 3 * D), D ** -0.5),
        "da_lam": nrm((N_DIFF, 4, DIFF_QK), 0.1),
        "da_subln_g": 1.0 + nrm((N_DIFF, DIFF_V), 0.05),
        "da_w_o": nrm((N_DIFF, D, D), BETA * D ** -0.5),
        "rel_bias": nrm((REL_BUCKETS, DIFF_HEADS), 0.5),
        "ff_w_up": nrm((DEPTH, D, 2 * F), D ** -0.5),
        "ff_conv_w": nrm((DEPTH, CONV_W, F), CONV_W ** -0.5),
        "ff_conv_b": nrm((DEPTH, F), 0.02),
        "ff_w_down": nrm((DEPTH, F, D), BETA * F ** -0.5),
    }


def reference(x, ln_g, ln_b, rw_mu, rw_w_rkv, rw_w0, rw_w1, rw_w2, rw_a0, rw_a1, rw_a2,
              rw_v0, rw_v1, rw_v2, rw_g1, rw_g2, rw_k_k, rw_k_a, rw_r_k, rw_lnx_g, rw_lnx_b, rw_w_o,
              da_w_qkv, da_lam, da_subln_g, da_w_o, rel_bias,
              ff_w_up, ff_conv_w, ff_conv_b, ff_w_down):
    v_first = None
    for i in range(DEPTH):
        j = i // N_MIXERS
        if i % N_MIXERS == 0:
            vres = None if j == 0 else (rw_v0[j - 1], rw_v1[j - 1], rw_v2[j - 1])
            h, v_first = rwkv7_mix(x, v_first, vres, rw_mu[j], rw_w_rkv[j], rw_w0[j], rw_w1[j], rw_w2[j],
                                   rw_a0[j], rw_a1[j], rw_a2[j], rw_g1[j], rw_g2[j], rw_k_k[j], rw_k_a[j],
                                   rw_r_k[j], rw_lnx_g[j], rw_lnx_b[j], rw_w_o[j])
        else:
            lambda_init = 0.8 - 0.6 * math.exp(-0.3 * i)
            h = diff_attn_mix(x, da_w_qkv[j], da_lam[j], da_subln_g[j], da_w_o[j], rel_bias, lambda_init)
        x = layer_norm(ALPHA * x + h, ln_g[i, 0], ln_b[i, 0])
        x = layer_norm(ALPHA * x + conv_glu(x, ff_w_up[i], ff_conv_w[i], ff_conv_b[i], ff_w_down[i]),
                       ln_g[i, 1], ln_b[i, 1])
    return x
```

```python
import contextlib
import math
import numpy as np
import concourse.bass as bass
import concourse.mybir as mybir
from concourse.bass_utils import run_bass_kernel_spmd

F32 = mybir.dt.float32
BF16 = mybir.dt.bfloat16
AF = mybir.ActivationFunctionType
ALU = mybir.AluOpType
AX = mybir.AxisListType

NCORES = 8
D = 2048
KC = D // 128
F = 5504
FC = F // 128
TT = 512
DEPTH = 4
ALPHA = (2 * DEPTH) ** 0.25
LN_EPS = 1e-5
SEQ = 8192
BATCH = 2

ENGINES = ("tensor", "vector", "scalar", "gpsimd", "sync")
NDMA_SEM = 12


class Op:
    __slots__ = ("eng", "fn", "deps", "dma", "sig", "waits", "has_dep", "idx")

    def __init__(self, eng, fn, dma):
        self.eng = eng
        self.fn = fn
        self.dma = dma
        self.deps = set()
        self.sig = None
        self.waits = []
        self.has_dep = False


class Prog:
    def __init__(self, nc):
        self.nc = nc
        self.ops = []
        self.last_w = {}
        self.readers = {}
        self.stack = contextlib.ExitStack()
        self.dma_slot_last = {}
        self.dma_count = {e: 0 for e in ENGINES}
        self.final = []
        self.tiles = {}

    def sb(self, name, shape, dt):
        if name not in self.tiles:
            self.tiles[name] = self.stack.enter_context(self.nc.sbuf_tensor("s_" + name, list(shape), dt))
        return self.tiles[name]

    def ps(self, name, shape, dt=F32):
        if name not in self.tiles:
            self.tiles[name] = self.stack.enter_context(self.nc.psum_tensor("p_" + name, list(shape), dt))
        return self.tiles[name]

    def dram(self, name, shape, dt, kind="Internal"):
        return self.nc.dram_tensor(name, list(shape), dt, kind=kind).ap()

    def op(self, eng, fn, r=(), w=(), dma=False):
        o = Op(eng, fn, dma)
        o.idx = len(self.ops)
        for k in r:
            lw = self.last_w.get(k)
            if lw is not None:
                o.deps.add(lw)
        for k in w:
            lw = self.last_w.get(k)
            if lw is not None:
                o.deps.add(lw)
            for rd in self.readers.get(k, ()):
                o.deps.add(rd)
        o.deps.discard(o.idx)
        for k in r:
            self.readers.setdefault(k, []).append(o.idx)
        for k in w:
            self.last_w[k] = o.idx
            self.readers[k] = []
        if dma:
            slot = (eng, self.dma_count[eng] % NDMA_SEM)
            self.dma_count[eng] += 1
            prev = self.dma_slot_last.get(slot)
            if prev is not None:
                o.deps.add(prev)
            self.dma_slot_last[slot] = o.idx
            o.sig = slot
        self.ops.append(o)
        return o.idx

    def dma(self, eng, out, in_, r=(), w=(), **kw):
        return self.op(eng, lambda e: e.dma_start(out=out, in_=in_, **kw), r, w, dma=True)

    def finish(self, op_ids):
        self.final.extend(op_ids)

    def emit(self):
        nc = self.nc
        ops = self.ops
        for o in ops:
            if o.eng == "tensor" and not o.dma:
                o.deps = {d for d in o.deps if not (ops[d].eng == "tensor" and not ops[d].dma)}
            for d in o.deps:
                ops[d].has_dep = True
        for i in self.final:
            ops[i].has_dep = True
        sems = {}
        for e in ENGINES:
            sems[("c", e)] = self.stack.enter_context(nc.semaphore("c_" + e))
        for slot in self.dma_slot_last:
            sems[slot] = self.stack.enter_context(nc.semaphore("d_%s_%d" % slot))
        cnt = {}
        for o in ops:
            if o.dma:
                cnt[o.sig] = cnt.get(o.sig, 0) + 16
                o.sig = (o.sig, cnt[o.sig], 16)
            elif o.has_dep:
                k = ("c", o.eng)
                cnt[k] = cnt.get(k, 0) + 1
                o.sig = (k, cnt[k], 1)
        waited = {}
        for o in ops:
            need = {}
            for d in o.deps:
                s, v, _ = ops[d].sig
                if v > need.get(s, 0):
                    need[s] = v
            for s, v in need.items():
                if waited.get((o.eng, s), 0) < v:
                    waited[(o.eng, s)] = v
                    o.waits.append((s, v))
        fin_need = {}
        for i in self.final:
            s, v, _ = ops[i].sig
            fin_need[s] = max(fin_need.get(s, 0), v)
        self.maxcnt = dict(cnt)
        with nc.Block() as block:
            for e in ENGINES:
                mine = [o for o in ops if o.eng == e]

                def body(eng, mine=mine, e=e):
                    for o in mine:
                        for s, v in o.waits:
                            eng.wait_ge(sems[s], v)
                        ins = o.fn(eng)
                        if o.sig is not None:
                            ins.then_inc(sems[o.sig[0]], o.sig[2])
                    if e == "sync":
                        for s, v in fin_need.items():
                            eng.wait_ge(sems[s], v)

                getattr(block, e)(body)
        self.stack.close()
        return nc


def chunkvec(v):
    v = np.asarray(v, np.float32)
    return np.ascontiguousarray(v.reshape(-1, 128).T)


def new_prog():
    nc = bass.Bass("TRN2", target_bir_lowering=False)
    P = Prog(nc)
    ones = P.sb("ones_bf", [128, 128], BF16)
    P.op("vector", lambda e: e.memset(ones[:], 1.0), w=["ones"])
    return nc, P


def emit_ln(P, tag, z, zkeys, par, gcol, bcol, outs, zb, zbk, zq, zqk):
    ones = P.tiles["ones_bf"]
    s1 = P.ps("bank4", [128, 512])
    s2 = P.ps("bank5", [128, 512])
    mean = P.sb("ln_mean", [128, TT], F32)
    rstd = P.sb("ln_rstd", [128, TT], F32)
    tmp = P.sb("ln_tmp", [128, 2, TT], F32)
    P.op("vector", lambda e: e.tensor_copy(zb, z), r=zkeys, w=zbk)
    P.op("scalar", lambda e: e.activation(zq, z, AF.Square), r=zkeys, w=zqk)
    for c in range(KC):
        P.op("tensor", lambda e, c=c: e.matmul(s1[:], ones[:], zb[:, c, :], start=(c == 0), stop=(c == KC - 1)),
             r=["ones"] + zbk, w=["bank4"])
    for c in range(KC):
        P.op("tensor", lambda e, c=c: e.matmul(s2[:], ones[:], zq[:, c, :], start=(c == 0), stop=(c == KC - 1)),
             r=["ones"] + zqk, w=["bank5"])
    P.op("vector", lambda e: e.tensor_scalar(mean[:], s1[:], 1.0 / D, None, ALU.mult), r=["bank4"], w=["ln_mean"])
    P.op("vector", lambda e: e.tensor_tensor(tmp[:, 0, :], mean[:], mean[:], ALU.mult), r=["ln_mean"], w=["ln_t0"])
    P.op("vector", lambda e: e.scalar_tensor_tensor(tmp[:, 1, :], s2[:], 1.0 / D, tmp[:, 0, :], ALU.mult, ALU.subtract),
         r=["bank5", "ln_t0"], w=["ln_t1"])
    P.op("vector", lambda e: e.tensor_scalar(tmp[:, 0, :], tmp[:, 1, :], LN_EPS, None, ALU.add), r=["ln_t1"], w=["ln_t0"])
    P.op("scalar", lambda e: e.activation(tmp[:, 1, :], tmp[:, 0, :], AF.Ln), r=["ln_t0"], w=["ln_t1"])
    P.op("scalar", lambda e: e.activation(rstd[:], tmp[:, 1, :], AF.Exp, scale=-0.5), r=["ln_t1"], w=["ln_rstd"])
    for c in range(KC):
        tt = P.sb("ln_tt%d" % (c % 2), [128, TT], F32)
        tk = "ln_tt%d" % (c % 2)
        P.op("gpsimd", lambda e, c=c, tt=tt: e.tensor_tensor(tt[:], z[:, c, :], mean[:], ALU.subtract),
             r=zkeys + ["ln_mean"], w=[tk])
        P.op("vector", lambda e, tt=tt: e.tensor_tensor(tt[:], tt[:], rstd[:], ALU.mult), r=[tk, "ln_rstd"], w=[tk])
        for (o, okey) in outs:
            P.op("scalar", lambda e, c=c, tt=tt, o=o: e.activation(
                o[:, c, :], tt[:], AF.Identity, bias=par[:, bcol + c:bcol + c + 1], scale=par[:, gcol + c:gcol + c + 1]),
                r=[tk, "par"], w=[okey + str(c)])


PAR_FF = FC * 4
FF_GRP = 2


def emit_ffn_tile(P, xf, xfkeys, wup, wdn, par, pcol, gcol, bcol, outs):
    xb = P.sb("ff_xb", [128, KC, 2 + TT], BF16)
    hT = P.sb("ff_hT", [128, FC, TT], BF16)
    wupv = wup.rearrange("(kc p) c -> p kc c", p=128)
    wdnv = wdn.rearrange("(fc p) d -> p fc d", p=128)
    P.op("vector", lambda e: e.tensor_copy(xb[:], xf[:]), r=xfkeys, w=["ff_xb"])
    ngrp = (FC + FF_GRP - 1) // FF_GRP
    for gi in range(ngrp):
        fc0 = gi * FF_GRP
        nfc = min(FF_GRP, FC - fc0)
        bi = gi % 2
        wu = P.sb("ff_wu%d" % bi, [128, KC, FF_GRP * 128], BF16)
        wg = P.sb("ff_wg%d" % bi, [128, KC, FF_GRP * 128], BF16)
        P.dma("gpsimd", wu[:, :, 0:nfc * 128], wupv[:, :, fc0 * 128:(fc0 + nfc) * 128], w=["ff_wu%d" % bi])
        P.dma("gpsimd", wg[:, :, 0:nfc * 128], wupv[:, :, F + fc0 * 128:F + (fc0 + nfc) * 128], w=["ff_wg%d" % bi])
        for j in range(nfc):
            fc = fc0 + j
            pb = fc % 2
            pu = P.ps("bank%d" % pb, [128, 512])
            pg = P.ps("bank%d" % (2 + pb), [128, 512])
            pgh = P.ps("bank6", [128, 512])[:, pb * 2:pb * 2 + 2]
            gpad = P.sb("ff_gpad%d" % pb, [128, 2 + TT], F32)
            t1 = P.sb("ff_t1%d" % pb, [128, TT], F32)
            ge = P.sb("ff_ge%d" % pb, [128, TT], F32)
            ku, kg, kgh = "bank%d" % pb, "bank%d" % (2 + pb), "bank6_%d" % pb
            kgp, kt1, kge = "ff_gpad%d" % pb, "ff_t1%d" % pb, "ff_ge%d" % pb
            for c in range(KC):
                P.op("tensor", lambda e, c=c, j=j, pu=pu, wu=wu: e.matmul(
                    pu[:], wu[:, c, j * 128:(j + 1) * 128], xb[:, c, 2:2 + TT], start=(c == 0), stop=(c == KC - 1)),
                    r=["ff_wu%d" % bi, "ff_xb"], w=[ku])
            for c in range(KC):
                P.op("tensor", lambda e, c=c, j=j, pg=pg, wg=wg: e.matmul(
                    pg[:], wg[:, c, j * 128:(j + 1) * 128], xb[:, c, 2:2 + TT], start=(c == 0), stop=(c == KC - 1)),
                    r=["ff_wg%d" % bi, "ff_xb"], w=[kg])
            for c in range(KC):
                P.op("tensor", lambda e, c=c, j=j, pgh=pgh, wg=wg: e.matmul(
                    pgh, wg[:, c, j * 128:(j + 1) * 128], xb[:, c, 0:2], start=(c == 0), stop=(c == KC - 1)),
                    r=["ff_wg%d" % bi, "ff_xb"], w=[kgh])
            P.op("scalar", lambda e, gpad=gpad, pgh=pgh: e.copy(gpad[:, 0:2], pgh), r=[kgh], w=[kgp + "h"])
            P.op("scalar", lambda e, gpad=gpad, pg=pg: e.copy(gpad[:, 2:2 + TT], pg[:]), r=[kg], w=[kgp])
            cw = lambda k, fc=fc: par[:, pcol + 4 * fc + k:pcol + 4 * fc + k + 1]
            P.op("vector", lambda e, gpad=gpad, t1=t1, cw=cw: e.tensor_scalar(
                t1[:], gpad[:, 0:TT], cw(0), cw(3), ALU.mult, ALU.add), r=[kgp, kgp + "h", "par"], w=[kt1])
            P.op("vector", lambda e, gpad=gpad, t1=t1, cw=cw: e.scalar_tensor_tensor(
                t1[:], gpad[:, 1:1 + TT], cw(1), t1[:], ALU.mult, ALU.add), r=[kgp, kgp + "h", "par", kt1], w=[kt1])
            P.op("vector", lambda e, gpad=gpad, t1=t1, cw=cw: e.scalar_tensor_tensor(
                t1[:], gpad[:, 2:2 + TT], cw(2), t1[:], ALU.mult, ALU.add), r=[kgp, "par", kt1], w=[kt1])
            P.op("scalar", lambda e, ge=ge, t1=t1: e.activation(ge[:], t1[:], AF.Gelu_apprx_tanh), r=[kt1], w=[kge])
            P.op("vector", lambda e, fc=fc, ge=ge, pu=pu: e.tensor_tensor(hT[:, fc, :], pu[:], ge[:], ALU.mult),
                 r=[ku, kge], w=["ff_hT%d" % fc])
    hkeys = ["ff_hT%d" % fc for fc in range(FC)]
    for dc in range(KC):
        bi = dc % 2
        wd = P.sb("ff_wd%d" % bi, [128, FC, 128], BF16)
        po = P.ps("bank%d" % (4 + bi), [128, 512])
        P.dma("gpsimd", wd[:], wdnv[:, :, dc * 128:(dc + 1) * 128], w=["ff_wd%d" % bi])
        for fc in range(FC):
            P.op("tensor", lambda e, fc=fc, wd=wd, po=po: e.matmul(
                po[:], wd[:, fc, :], hT[:, fc, :], start=(fc == 0), stop=(fc == FC - 1)),
                r=["ff_wd%d" % bi] + hkeys, w=["bank%d" % (4 + bi)])
        P.op("vector", lambda e, dc=dc, po=po: e.scalar_tensor_tensor(
            xf[:, dc, 2:2 + TT], xf[:, dc, 2:2 + TT], ALPHA, po[:], ALU.mult, ALU.add),
            r=xfkeys + ["bank%d" % (4 + bi)], w=xfkeys)
    emit_ln(P, "ln", xf[:, :, 2:2 + TT], xfkeys, par, gcol, bcol, outs,
            xb[:, :, 0:TT], ["ff_xb"], hT[:, 0:KC, :], hkeys[0:KC])


def build_ffn(NT):
    nc, P = new_prog()
    xin = P.dram("x", [D, 2 + NT * TT], F32, "ExternalInput")
    wup = P.dram("wup", [D, 2 * F], F32, "ExternalInput")
    wdn = P.dram("wdn", [F, D], F32, "ExternalInput")
    parD = P.dram("par", [128, PAR_FF + 2 * KC], F32, "ExternalInput")
    out = P.dram("out", [D, NT * TT], F32, "ExternalOutput")
    xv = xin.rearrange("(kc p) t -> p kc t", p=128)
    ov = out.rearrange("(kc p) t -> p kc t", p=128)
    par = P.sb("par", [128, PAR_FF + 2 * KC], F32)
    P.dma("sync", par[:], parD, w=["par"])
    fin = []
    for ti in range(NT):
        xf = P.sb("xf", [128, KC, 2 + TT], F32)
        of = P.sb("of", [128, KC, TT], F32)
        P.dma("sync", xf[:], xv[:, :, ti * TT:ti * TT + 2 + TT], w=["xf"])
        emit_ffn_tile(P, xf, ["xf"], wup, wdn, par, 0, PAR_FF, PAR_FF + KC, [(of, "of")])
        fin.append(P.dma("sync", ov[:, :, ti * TT:(ti + 1) * TT], of[:], r=["of%d" % c for c in range(KC)]))
    P.finish(fin)
    P.emit()
    return nc


def ffn_params(conv_w, conv_b, ln_g, ln_b):
    cw = [chunkvec(conv_w[k]) for k in range(3)] + [chunkvec(conv_b)]
    pf = np.stack(cw, axis=2).reshape(128, FC * 4)
    return np.ascontiguousarray(np.concatenate([pf, chunkvec(ln_g), chunkvec(ln_b)], axis=1))


def emit_wblock(P, name, wview, col0, ncols, bi, rows=KC):
    wt = P.sb("%s%d" % (name, bi), [128, rows, 128], BF16)
    key = "%s%d" % (name, bi)
    P.dma("gpsimd", wt[:, :, 0:ncols], wview[:, :, col0:col0 + ncols], w=[key])
    return wt, key


def build_proj(NT, NCOLS, mode):
    nc, P = new_prog()
    xin = P.dram("x", [D, NT * TT], F32, "ExternalInput").rearrange("(kc p) t -> p kc t", p=128)
    w = P.dram("w", [D, NCOLS], F32, "ExternalInput").rearrange("(kc p) c -> p kc c", p=128)
    OC = NCOLS // 128
    out = P.dram("out", [NCOLS, NT * TT], F32, "ExternalOutput").rearrange("(oc p) t -> p oc t", p=128)
    if mode == "resln":
        res = P.dram("res", [D, NT * TT], F32, "ExternalInput").rearrange("(kc p) t -> p kc t", p=128)
        parD = P.dram("par", [128, 2 * KC], F32, "ExternalInput")
        par = P.sb("par", [128, 2 * KC], F32)
        P.dma("sync", par[:], parD, w=["par"])
    fin = []
    if mode == "plain":
        xall = P.sb("pj_xall", [128, KC, NT * TT], BF16)
        for ti in range(NT):
            P.dma("gpsimd", xall[:, :, ti * TT:(ti + 1) * TT], xin[:, :, ti * TT:(ti + 1) * TT], w=["pj_xall%d" % ti])
        n = 0
        for oc in range(OC):
            wt, wk = emit_wblock(P, "pj_w", w, oc * 128, 128, oc % 2)
            for ti in range(NT):
                tsl = slice(ti * TT, (ti + 1) * TT)
                bi = n % 4
                n += 1
                ps = P.ps("bank%d" % bi, [128, 512])
                pk = "bank%d" % bi
                for c in range(KC):
                    P.op("tensor", lambda e, c=c, ps=ps, wt=wt, tsl=tsl: e.matmul(ps[:], wt[:, c, :], xall[:, c, tsl], start=(c == 0), stop=(c == KC - 1)),
                         r=[wk, "pj_xall%d" % ti], w=[pk])
                ob = P.sb("pj_ob%d" % bi, [128, TT], F32)
                ok = "pj_ob%d" % bi
                if n % 2 == 0:
                    P.op("scalar", lambda e, ob=ob, ps=ps: e.copy(ob[:], ps[:]), r=[pk], w=[ok])
                else:
                    P.op("vector", lambda e, ob=ob, ps=ps: e.tensor_copy(ob[:], ps[:]), r=[pk], w=[ok])
                fin.append(P.dma("sync", out[:, oc, tsl], ob[:], r=[ok]))
        P.finish(fin)
        P.emit()
        return nc
    for ti in range(NT):
        tsl = slice(ti * TT, (ti + 1) * TT)
        xb = P.sb("pj_xb", [128, KC, TT], BF16)
        P.dma("gpsimd", xb[:], xin[:, :, tsl], w=["pj_xb"])
        if mode == "resln":
            zf = P.sb("pj_zf", [128, KC, TT], F32)
            P.dma("sync", zf[:], res[:, :, tsl], w=["pj_zf"])
        for oc in range(OC):
            wt, wk = emit_wblock(P, "pj_w", w, oc * 128, 128, oc % 2)
            ps = P.ps("bank%d" % (oc % 2), [128, 512])
            pk = "bank%d" % (oc % 2)
            for c in range(KC):
                P.op("tensor", lambda e, c=c, ps=ps, wt=wt: e.matmul(ps[:], wt[:, c, :], xb[:, c, :], start=(c == 0), stop=(c == KC - 1)),
                     r=[wk, "pj_xb"], w=[pk])
            if mode == "plain":
                ob = P.sb("pj_ob%d" % (oc % 2), [128, TT], F32)
                ok = "pj_ob%d" % (oc % 2)
                P.op("scalar", lambda e, ob=ob, ps=ps: e.copy(ob[:], ps[:]), r=[pk], w=[ok])
                fin.append(P.dma("sync", out[:, oc, tsl], ob[:], r=[ok]))
            else:
                P.op("vector", lambda e, oc=oc, ps=ps: e.scalar_tensor_tensor(zf[:, oc, :], zf[:, oc, :], ALPHA, ps[:], ALU.mult, ALU.add),
                     r=["pj_zf", pk], w=["pj_zf"])
        if mode == "resln":
            of = P.sb("pj_of", [128, KC, TT], F32)
            zb = P.sb("pj_zb", [128, KC, TT], BF16)
            zq = P.sb("pj_zq", [128, KC, TT], BF16)
            emit_ln(P, "ln", zf[:], ["pj_zf"], par, 0, KC, [(of, "pj_of")], zb[:], ["pj_zb"], zq[:], ["pj_zq"])
            fin.append(P.dma("sync", out[:, :, tsl], of[:], r=["pj_of%d" % c for c in range(KC)]))
    P.finish(fin)
    P.emit()
    return nc


PRE_NPAR = 11 * KC
LW_SCALE = -math.exp(-0.5)


def build_pre(NT, vres):
    nc, P = new_prog()
    xin = P.dram("x", [D, 1 + NT * TT], F32, "ExternalInput").rearrange("(kc p) t -> p kc t", p=128)
    wbig = {n: P.dram(n, [D, D], F32, "ExternalInput").rearrange("(kc p) c -> p kc c", p=128) for n in ("wr", "wk", "wv")}
    lo1 = {"w1": 96, "a1": 96, "g1": 128}
    if vres:
        lo1["v1"] = 64
    l1d = {n: P.dram(n, [D, m], F32, "ExternalInput").rearrange("(kc p) c -> p kc c", p=128) for n, m in lo1.items()}
    l2d = {n.replace("1", "2"): P.dram(n.replace("1", "2"), [m, D], F32, "ExternalInput") for n, m in lo1.items()}
    parD = P.dram("par", [128, PRE_NPAR], F32, "ExternalInput")
    if vres:
        vfd = P.dram("vfirst", [D, NT * TT], F32, "ExternalInput").rearrange("(kc p) t -> p kc t", p=128)
    onames = ("r", "k", "v", "a", "b", "lw", "g")
    outs = {n: P.dram("o_" + n, [D, NT * TT], F32, "ExternalOutput").rearrange("(kc p) t -> p kc t", p=128) for n in onames}
    par = P.sb("par", [128, PRE_NPAR], F32)
    P.dma("sync", par[:], parD, w=["par"])
    pc = lambda grp, c: par[:, grp * KC + c:grp * KC + c + 1]
    l1 = {}
    l2 = {}
    for n, m in lo1.items():
        l1[n] = P.sb("l_" + n, [128, KC, m], BF16)
        P.dma("gpsimd", l1[n][:], l1d[n], w=["l_" + n])
        n2 = n.replace("1", "2")
        l2[n2] = P.sb("l_" + n2, [m, D], BF16)
        P.dma("gpsimd", l2[n2][:], l2d[n2], w=["l_" + n2])
    blk = P.sb("blk64", [128, 128], BF16)
    P.op("vector", lambda e: e.memset(blk[:], 0.0), w=["blk64"])
    P.op("vector", lambda e: e.memset(blk[0:64, 0:64], 1.0), w=["blk64"])
    P.op("vector", lambda e: e.memset(blk[64:128, 64:128], 1.0), w=["blk64"])
    fin = []
    cnt = [0]

    def mix(i, xf, xx, name):
        xm = P.sb("xm_" + name, [128, KC, TT], BF16)
        for c in range(KC):
            eng = "vector"
            P.op(eng, lambda e, c=c: e.scalar_tensor_tensor(xm[:, c, :], xx[:, c, :], pc(i, c), xf[:, c, 1:1 + TT], ALU.mult, ALU.add),
                 r=["xx", "xf", "par"], w=["xm_%s%d" % (name, c)])
        return xm, ["xm_%s%d" % (name, c) for c in range(KC)]

    def proj_chunk(wname, oc, xm, xmk):
        bi = cnt[0] % 2
        cnt[0] += 1
        wt, wk = emit_wblock(P, "pw", wbig[wname], oc * 128, 128, bi)
        ps = P.ps("bank%d" % bi, [128, 512])
        for c in range(KC):
            P.op("tensor", lambda e, c=c, ps=ps, wt=wt: e.matmul(ps[:], wt[:, c, :], xm[:, c, :], start=(c == 0), stop=(c == KC - 1)),
                 r=[wk] + xmk, w=["bank%d" % bi])
        return ps, "bank%d" % bi

    def hidden(n1, xm, xmk, func):
        m = lo1[n1]
        ps = P.ps("bank2", [128, 512])
        hb = P.sb("hid_" + n1, [m, TT], BF16)
        for c in range(KC):
            P.op("tensor", lambda e, c=c: e.matmul(ps[0:m, :], l1[n1][:, c, :], xm[:, c, :], start=(c == 0), stop=(c == KC - 1)),
                 r=["l_" + n1] + xmk, w=["bank2"])
        if func is None:
            P.op("scalar", lambda e: e.copy(hb[:], ps[0:m, :]), r=["bank2"], w=["hid_" + n1])
        else:
            P.op("scalar", lambda e: e.activation(hb[:], ps[0:m, :], func), r=["bank2"], w=["hid_" + n1])
        return hb, "hid_" + n1

    def lora_out(n2, hb, hk, oc):
        m = lo1[n2.replace("2", "1")]
        ps = P.ps("bank3", [128, 512])
        P.op("tensor", lambda e: e.matmul(ps[:], l2[n2][0:m, oc * 128:(oc + 1) * 128], hb[0:m, :], start=True, stop=True),
             r=["l_" + n2, hk], w=["bank3"])
        return ps, "bank3"

    def obuf(tag):
        bi = cnt[0] % 2
        cnt[0] += 1
        return P.sb("ob_%s%d" % (tag, bi), [128, TT], F32), "ob_%s%d" % (tag, bi)

    for ti in range(NT):
        tsl = slice(ti * TT, (ti + 1) * TT)
        xf = P.sb("xf", [128, KC, 1 + TT], F32)
        xx = P.sb("xx", [128, KC, TT], F32)
        P.dma("sync", xf[:], xin[:, :, ti * TT:ti * TT + 1 + TT], w=["xf"])
        P.op("vector", lambda e: e.tensor_tensor(xx[:], xf[:, :, 0:TT], xf[:, :, 1:1 + TT], ALU.subtract), r=["xf"], w=["xx"])
        xm, xmk = mix(0, xf, xx, "A")
        for oc in range(KC):
            ps, pk = proj_chunk("wr", oc, xm, xmk)
            ob, ok = obuf("r")
            P.op("scalar", lambda e, ob=ob, ps=ps: e.copy(ob[:], ps[:]), r=[pk], w=[ok])
            fin.append(P.dma("sync", outs["r"][:, oc, tsl], ob[:], r=[ok]))
        xm, xmk = mix(1, xf, xx, "B")
        hb, hk = hidden("w1", xm, xmk, AF.Tanh)
        for oc in range(KC):
            ps, pk = lora_out("w2", hb, hk, oc)
            ob, ok = obuf("lw")
            P.op("scalar", lambda e, ob=ob, ps=ps, oc=oc: e.activation(ob[:], ps[:], AF.Sigmoid, bias=pc(6, oc)), r=[pk, "par"], w=[ok])
            P.op("gpsimd", lambda e, ob=ob: e.tensor_scalar(ob[:], ob[:], LW_SCALE, None, ALU.mult), r=[ok], w=[ok])
            fin.append(P.dma("sync", outs["lw"][:, oc, tsl], ob[:], r=[ok]))
        xm, xmk = mix(5, xf, xx, "A")
        hb, hk = hidden("g1", xm, xmk, AF.Sigmoid)
        for oc in range(KC):
            ps, pk = lora_out("g2", hb, hk, oc)
            ob, ok = obuf("g")
            P.op("scalar", lambda e, ob=ob, ps=ps: e.copy(ob[:], ps[:]), r=[pk], w=[ok])
            fin.append(P.dma("sync", outs["g"][:, oc, tsl], ob[:], r=[ok]))
        xm, xmk = mix(3, xf, xx, "B")
        if vres:
            hb, hk = hidden("v1", xm, xmk, None)
        for oc in range(KC):
            ps, pk = proj_chunk("wv", oc, xm, xmk)
            ob, ok = obuf("v")
            P.op("scalar", lambda e, ob=ob, ps=ps: e.copy(ob[:], ps[:]), r=[pk], w=[ok])
            if vres:
                ps2, pk2 = lora_out("v2", hb, hk, oc)
                sv, svk = obuf("sv")
                vf, vfk = obuf("vf")
                P.op("scalar", lambda e, sv=sv, ps2=ps2, oc=oc: e.activation(sv[:], ps2[:], AF.Sigmoid, bias=pc(8, oc)), r=[pk2, "par"], w=[svk])
                P.dma("sync", vf[:], vfd[:, oc, tsl], w=[vfk])
                P.op("gpsimd", lambda e, vf=vf, ob=ob: e.tensor_tensor(vf[:], vf[:], ob[:], ALU.subtract), r=[vfk, ok], w=[vfk])
                P.op("vector", lambda e, vf=vf, sv=sv: e.tensor_tensor(vf[:], vf[:], sv[:], ALU.mult), r=[vfk, svk], w=[vfk])
                P.op("gpsimd", lambda e, vf=vf, ob=ob: e.tensor_tensor(ob[:], ob[:], vf[:], ALU.add), r=[vfk, ok], w=[ok])
            fin.append(P.dma("sync", outs["v"][:, oc, tsl], ob[:], r=[ok]))
        xmk_, xmkk = mix(2, xf, xx, "A")
        xma, xmak = mix(4, xf, xx, "B")
        hb, hk = hidden("a1", xma, xmak, None)
        for oc in range(KC):
            ps, pk = proj_chunk("wk", oc, xmk_, xmkk)
            ps2, pk2 = lora_out("a2", hb, hk, oc)
            ag, agk = obuf("ag")
            kf, kfk = obuf("kf")
            kk, kkk = obuf("kk")
            P.op("scalar", lambda e, ag=ag, ps2=ps2, oc=oc: e.activation(ag[:], ps2[:], AF.Sigmoid, bias=pc(7, oc)), r=[pk2, "par"], w=[agk])
            P.op("scalar", lambda e, kf=kf, ps=ps: e.copy(kf[:], ps[:]), r=[pk], w=[kfk])
            P.op("vector", lambda e, kk=kk, kf=kf, oc=oc: e.tensor_scalar(kk[:], kf[:], pc(9, oc), None, ALU.mult), r=[kfk, "par"], w=[kkk])
            kq = P.sb("kq", [128, TT], BF16)
            P.op("scalar", lambda e, kk=kk: e.activation(kq[:], kk[:], AF.Square), r=[kkk], w=["kq"])
            pn = P.ps("bank4", [128, 512])
            P.op("tensor", lambda e: e.matmul(pn[:], blk[:], kq[:], start=True, stop=True), r=["blk64", "kq"], w=["bank4"])
            inv, invk = obuf("inv")
            P.op("vector", lambda e, inv=inv: e.tensor_scalar(inv[:], pn[:], 1e-24, None, ALU.max), r=["bank4"], w=[invk])
            P.op("scalar", lambda e, inv=inv: e.activation(inv[:], inv[:], AF.Ln), r=[invk], w=[invk])
            P.op("scalar", lambda e, inv=inv: e.activation(inv[:], inv[:], AF.Exp, scale=-0.5), r=[invk], w=[invk])
            P.op("vector", lambda e, kk=kk, inv=inv: e.tensor_tensor(kk[:], kk[:], inv[:], ALU.mult), r=[kkk, invk], w=[kkk])
            oa, oak = obuf("oa")
            obb, obk = obuf("obb")
            P.op("gpsimd", lambda e, oa=oa, kk=kk: e.tensor_scalar(oa[:], kk[:], -1.0, None, ALU.mult), r=[kkk], w=[oak])
            fin.append(P.dma("sync", outs["a"][:, oc, tsl], oa[:], r=[oak]))
            P.op("vector", lambda e, obb=obb, kk=kk, ag=ag: e.tensor_tensor(obb[:], kk[:], ag[:], ALU.mult), r=[kkk, agk], w=[obk])
            fin.append(P.dma("sync", outs["b"][:, oc, tsl], obb[:], r=[obk]))
            P.op("gpsimd", lambda e, ag=ag: e.tensor_scalar(ag[:], ag[:], -1.0, None, ALU.add), r=[agk], w=[agk])
            P.op("vector", lambda e, ag=ag, oc=oc: e.tensor_scalar(ag[:], ag[:], pc(10, oc), None, ALU.mult), r=[agk, "par"], w=[agk])
            P.op("vector", lambda e, ag=ag, kf=kf: e.scalar_tensor_tensor(kf[:], ag[:], 1.0, kf[:], ALU.add, ALU.mult), r=[agk, kfk], w=[kfk])
            fin.append(P.dma("sync", outs["k"][:, oc, tsl], kf[:], r=[kfk]))
    P.finish(fin)
    P.emit()
    return nc


def pre_params(mu, w0, a0, v0, k_k, k_a):
    cols = [chunkvec(mu[i]) for i in range(6)] + [chunkvec(w0), chunkvec(a0), chunkvec(v0), chunkvec(k_k), chunkvec(k_a)]
    return np.ascontiguousarray(np.concatenate(cols, axis=1))


CS = 128
SCAN_TS = 256
INV_DT = BF16


def emit_scan_consts(P):
    dm = P.sb("c_dm", [128, 128], F32)
    dmi = P.sb("c_dmi", [128, 128], mybir.dt.int32)
    P.op("gpsimd", lambda e: e.iota(dmi[:], [[1, 128]], base=0, channel_multiplier=-1), w=["c_dmi"])
    P.op("vector", lambda e: e.tensor_copy(dm[:], dmi[:]), r=["c_dmi"], w=["c_dm"])
    m4 = P.sb("c_m4", [128, 512], F32)
    ml = P.sb("c_ml", [128, 128], F32)
    i128 = P.sb("c_i128", [128, 128], F32)
    i64b = P.sb("c_i64b", [64, 64], BF16)
    for q in range(4):
        P.op("vector", lambda e, q=q: e.tensor_scalar(m4[:, q * 128:(q + 1) * 128], dm[:], 0.0, None,
                                                    ALU.is_gt if q % 2 == 0 else ALU.is_ge), r=["c_dm"], w=["c_m4_%d" % q])
    P.op("vector", lambda e: e.tensor_scalar(ml[:], dm[:], 0.0, None, ALU.is_lt), r=["c_dm"], w=["c_ml"])
    P.op("vector", lambda e: e.tensor_scalar(i128[:], dm[:], 0.0, None, ALU.is_equal), r=["c_dm"], w=["c_i128"])
    P.op("vector", lambda e: e.tensor_copy(i64b[:], i128[0:64, 0:64]), r=["c_i128"], w=["c_i64b"])
    rm = P.sb("c_rm", [64, SCAN_TS], F32)
    P.op("vector", lambda e: e.memset(rm[:], 1.0), w=["c_rm"])
    for c in range(SCAN_TS // CS):
        P.op("vector", lambda e, c=c: e.memset(rm[:, c * CS:c * CS + 1], 0.0), w=["c_rm"])
    return ["c_m4_%d" % q for q in range(4)]


def build_scan(NH, T, post=False):
    TS = SCAN_TS
    NCH = TS // CS
    nc, P = new_prog()
    names = ("r", "k", "v", "a", "b", "lw")
    ins = {n: P.dram(n, [NH * 64, T], F32, "ExternalInput").rearrange("(h p) t -> p h t", p=64) for n in names}
    yout = P.dram("y", [NH * 64, T], F32, "ExternalOutput").rearrange("(h p) t -> p h t", p=64)
    m4keys = emit_scan_consts(P)
    if post:
        gin = P.dram("g", [NH * 64, T], F32, "ExternalInput").rearrange("(h p) t -> p h t", p=64)
        gparD = P.dram("gpar", [64, NH * 3], F32, "ExternalInput")
        gpar = P.sb("gpar", [64, NH * 3], F32)
        P.dma("sync", gpar[:], gparD, w=["gpar"])
        ones64 = P.sb("ones64", [64, 64], BF16)
        P.op("vector", lambda e: e.memset(ones64[:], 1.0), w=["ones64"])
    m4, ml, i128, i64b, rm = (P.tiles[k] for k in ("c_m4", "c_ml", "c_i128", "c_i64b", "c_rm"))
    Hf = P.sb("Hf", [64, NH, 64], F32)
    Hb = P.sb("Hb", [64, NH, 64], BF16)
    P.op("vector", lambda e: e.memset(Hf[:], 0.0), w=["Hf%d" % h for h in range(NH)])
    P.op("vector", lambda e: e.memset(Hb[:], 0.0), w=["Hb%d" % h for h in range(NH)])
    cw = P.sb("cw", [64, NH, TS], F32)
    cwx = P.sb("cwx", [64, NH, TS], F32)
    Ep = P.sb("Ep", [64, NH, TS], F32)
    Em = P.sb("Em", [64, NH, TS], F32)
    Ex = P.sb("Ex", [64, NH, TS], F32)
    At, Rt, Kt, Bt, Vb = (P.sb(n, [64, NH, TS], BF16) for n in ("At", "Rt", "Kt", "Bt", "Vb"))
    fin = []
    u = 0
    for ti in range(T // TS):
        tl = {}
        bsel = lambda n: 0 if n in ("lw", "a", "b") else ti % 2
        for n in names:
            tl[n] = P.sb("in_%s%d" % (n, bsel(n)), [64, NH, TS], F32)
            P.dma("sync", tl[n][:], ins[n][:, :, ti * TS:(ti + 1) * TS], w=["in_%s%d" % (n, bsel(n))])
        ik = lambda n, ti=ti: "in_%s%d" % (n, 0 if n in ("lw", "a", "b") else ti % 2)
        for h in range(NH):
            P.op("vector", lambda e, h=h, lw=tl["lw"]: e.tensor_tensor_scan(cw[:, h, :], rm[:], lw[:, h, :], 0.0, ALU.mult, ALU.add),
                 r=["c_rm", ik("lw")], w=["cw%d" % h])
        cwk = ["cw%d" % h for h in range(NH)]
        P.op("gpsimd", lambda e, lw=tl["lw"]: e.tensor_tensor(cwx[:], cw[:], lw[:], ALU.subtract), r=cwk + [ik("lw")], w=["cwx"])
        P.op("scalar", lambda e: e.activation(Ep[:], cw[:], AF.Exp), r=cwk, w=["Ep"])
        P.op("scalar", lambda e: e.activation(Em[:], cw[:], AF.Exp, scale=-1.0), r=cwk, w=["Em"])
        P.op("scalar", lambda e: e.activation(Ex[:], cwx[:], AF.Exp), r=["cwx"], w=["Ex"])
        P.op("vector", lambda e, x=tl["r"]: e.tensor_tensor(Rt[:], x[:], Ep[:], ALU.mult), r=[ik("r"), "Ep"], w=["Rt"])
        P.op("gpsimd", lambda e, x=tl["k"]: e.tensor_tensor(Kt[:], x[:], Em[:], ALU.mult), r=[ik("k"), "Em"], w=["Kt"])
        P.op("vector", lambda e, x=tl["b"]: e.tensor_tensor(Bt[:], x[:], Em[:], ALU.mult), r=[ik("b"), "Em"], w=["Bt"])
        P.op("gpsimd", lambda e, x=tl["a"]: e.tensor_tensor(At[:], x[:], Ex[:], ALU.mult), r=[ik("a"), "Ex"], w=["At"])
        P.op("scalar", lambda e, x=tl["v"]: e.copy(Vb[:], x[:]), r=[ik("v")], w=["Vb"])
        yt = P.sb("yt0", [64, NH, TS], F32)
        ytk = "yt0"
        for c in range(NCH):
            cs = slice(c * CS, (c + 1) * CS)
            for h in range(NH):
                ub = u % 4
                u += 1
                trp = P.ps("bank0", [128, 512])
                sc = P.ps("bank1", [128, 512])
                scl = P.ps("bank0", [128, 512])
                pinv = P.ps("bank%d" % (2, 3, 4, 6)[ub], [128, 512])
                kpinv = "bank%d" % (2, 3, 4, 6)[ub]
                xps = P.ps("bank5", [128, 512])
                yps = P.ps("bank5", [128, 512])
                hps = P.ps("bank7", [128, 512])
                tok = P.sb("tok%d" % ub, [128, 192], BF16)
                scm = P.sb("scm%d" % ub, [128, 512], BF16)
                ktok, kscm = "tok%d" % ub, "scm%d" % ub
                for q, (src, sk) in enumerate(((Kt, "Kt"), (Bt, "Bt"), (Vb, "Vb"))):
                    P.op("tensor", lambda e, q=q, src=src, h=h, cs=cs, trp=trp: e.matmul(
                        trp[:, q * 64:(q + 1) * 64], src[:, h, cs], i64b[:], start=True, stop=True),
                        r=[sk, "c_i64b"], w=["bank0"])
                P.op("scalar", lambda e, tok=tok, trp=trp: e.copy(tok[:], trp[:, 0:192]), r=["bank0"], w=[ktok])
                for q, (lh, rh, lk, rk) in enumerate(((Bt, At, "Bt", "At"), (Bt, Rt, "Bt", "Rt"), (Kt, At, "Kt", "At"), (Kt, Rt, "Kt", "Rt"))):
                    P.op("tensor", lambda e, q=q, lh=lh, rh=rh, h=h, cs=cs, sc=sc: e.matmul(
                        sc[:, q * 128:(q + 1) * 128], lh[:, h, cs], rh[:, h, cs], start=True, stop=True),
                        r=[lk, rk], w=["bank1"])
                P.op("vector", lambda e, scm=scm, sc=sc: e.tensor_tensor(scm[:], sc[:], m4[:], ALU.mult),
                     r=["bank1"] + m4keys, w=[kscm])
                PT = P.sb("PTi%d" % ub, [128, 128], INV_DT)[:]
                Pm = P.sb("Pmi%d" % ub, [128, 128], INV_DT)[:]
                QT = P.sb("QTi%d" % ub, [128, 128], INV_DT)[:]
                kPT, kPm, kQT = "PTi%d" % ub, "Pmi%d" % ub, "QTi%d" % ub
                P.op("vector", lambda e, PT=PT, sc=sc: e.tensor_tensor(PT, sc[:, 0:128], m4[:, 0:128], ALU.mult),
                     r=["bank1", m4keys[0]], w=[kPT])
                P.op("gpsimd", lambda e, PT=PT, QT=QT: e.tensor_tensor(QT, PT, i128[:], ALU.add),
                     r=[kPT, "c_i128"], w=[kQT])
                P.op("tensor", lambda e, h=h, cs=cs, scl=scl: e.matmul(
                    scl[:, 256:384], At[:, h, cs], Bt[:, h, cs], start=True, stop=True), r=["At", "Bt"], w=["bank0"])
                P.op("vector", lambda e, Pm=Pm, scl=scl: e.tensor_tensor(Pm, scl[:, 256:384], ml[:], ALU.mult),
                     r=["bank0", "c_ml"], w=[kPm])
                for lev in range(1, 7):
                    pp = lev % 2
                    PPn = P.sb("PP%d_%d" % (ub, pp), [128, 256], INV_DT)
                    Pn = PPn[:, 0:128]
                    PTn = PPn[:, 128:256]
                    QTn = P.sb("QT%d_%d" % (ub, pp), [128, 128], INV_DT)
                    kPn = kPTn = "PP%d_%d" % (ub, pp)
                    kQTn = "QT%d_%d" % (ub, pp)
                    P.op("tensor", lambda e, pinv=pinv, PT=PT, Pm=Pm: e.matmul(pinv[:, 0:128], PT, Pm, start=True, stop=True),
                         r=[kPT, kPm], w=[kpinv])
                    if lev < 6:
                        P.op("tensor", lambda e, pinv=pinv, PT=PT, Pm=Pm: e.matmul(pinv[:, 128:256], Pm, PT, start=True, stop=True),
                             r=[kPT, kPm], w=[kpinv])
                    P.op("scalar", lambda e, PPn=PPn, pinv=pinv: e.copy(PPn[:], pinv[:, 0:256]), r=[kpinv], w=[kPn])
                    P.op("tensor", lambda e, pinv=pinv, Pn=Pn, QT=QT: e.matmul(pinv[:, 256:384], Pn, QT, start=True, stop=True),
                         r=[kPn, kQT], w=[kpinv])
                    P.op("vector", lambda e, QTn=QTn, QT=QT, pinv=pinv: e.tensor_tensor(QTn[:], pinv[:, 256:384], QT, ALU.add),
                         r=[kpinv, kQT], w=[kQTn])
                    Pm, PT, QT = Pn, PTn, QTn[:]
                    kPm, kPT, kQT = kPn, kPTn, kQTn
                QTb = P.sb("QTb%d" % ub, [128, 128], BF16)
                P.op("scalar", lambda e, QTb=QTb, QT=QT: e.copy(QTb[:], QT), r=[kQT], w=["QTb%d" % ub])
                Xb = P.sb("Xb%d" % ub, [128, 64], BF16)
                Ub = P.sb("Ub%d" % ub, [128, 64], BF16)
                P.op("tensor", lambda e, h=h, cs=cs, xps=xps: e.matmul(xps[:, 0:64], At[:, h, cs], Hb[:, h, :], start=True, stop=False),
                     r=["At", "Hb%d" % h], w=["bank5"])
                P.op("tensor", lambda e, xps=xps, scm=scm, tok=tok: e.matmul(xps[:, 0:64], scm[:, 256:384], tok[:, 128:192], start=False, stop=True),
                     r=[kscm, ktok], w=["bank5"])
                P.op("vector", lambda e, Xb=Xb, xps=xps: e.tensor_copy(Xb[:], xps[:, 0:64]), r=["bank5"], w=["Xb%d" % ub])
                P.op("tensor", lambda e, xps=xps, QTb=QTb, Xb=Xb: e.matmul(xps[:, 64:128], QTb[:], Xb[:], start=True, stop=True),
                     r=["QTb%d" % ub, "Xb%d" % ub], w=["bank5"])
                P.op("vector", lambda e, Ub=Ub, xps=xps: e.tensor_copy(Ub[:], xps[:, 64:128]), r=["bank5"], w=["Ub%d" % ub])
                P.op("tensor", lambda e, h=h, cs=cs, yps=yps: e.matmul(yps[0:64, 128:256], Hb[:, h, :], Rt[:, h, cs], start=True, stop=False),
                     r=["Hb%d" % h, "Rt"], w=["bank5"])
                P.op("tensor", lambda e, yps=yps, tok=tok, scm=scm: e.matmul(yps[0:64, 128:256], tok[:, 128:192], scm[:, 384:512], start=False, stop=False),
                     r=[ktok, kscm], w=["bank5"])
                P.op("tensor", lambda e, yps=yps, Ub=Ub, scm=scm: e.matmul(yps[0:64, 128:256], Ub[:], scm[:, 128:256], start=False, stop=True),
                     r=["Ub%d" % ub, kscm], w=["bank5"])
                P.op("scalar", lambda e, h=h, cs=cs, yt=yt, yps=yps: e.copy(yt[:, h, cs], yps[0:64, 128:256]), r=["bank5"], w=[ytk + "_%d_%d" % (h, c)])
                P.op("tensor", lambda e, hps=hps, tok=tok: e.matmul(hps[0:64, 0:64], tok[:, 0:64], tok[:, 128:192], start=True, stop=False),
                     r=[ktok], w=["bank7"])
                P.op("tensor", lambda e, hps=hps, tok=tok, Ub=Ub: e.matmul(hps[0:64, 0:64], tok[:, 64:128], Ub[:], start=False, stop=True),
                     r=[ktok, "Ub%d" % ub], w=["bank7"])
                htmp = P.sb("htmp%d" % ub, [64, 64], F32)
                P.op("vector", lambda e, h=h, htmp=htmp, hps=hps: e.tensor_tensor(htmp[:], hps[0:64, 0:64], Hf[:, h, :], ALU.add),
                     r=["bank7", "Hf%d" % h], w=["htmp%d" % ub])
                P.op("vector", lambda e, h=h, c=c, htmp=htmp: e.tensor_scalar(
                    Hf[:, h, :], htmp[:], Ep[:, h, c * CS + CS - 1:c * CS + CS], None, ALU.mult),
                    r=["htmp%d" % ub, "Ep"], w=["Hf%d" % h])
                P.op("scalar", lambda e, h=h: e.copy(Hb[:, h, :], Hf[:, h, :]), r=["Hf%d" % h], w=["Hb%d" % h])
        ykeys = [ytk + "_%d_%d" % (h, c) for h in range(NH) for c in range(NCH)]
        if post:
            gt = P.sb("g_in", [64, NH, TS], F32)
            P.dma("sync", gt[:], gin[:, :, ti * TS:(ti + 1) * TS], w=["g_in"])
            yb = P.sb("yb", [64, NH, TS], BF16)
            yq = P.sb("yq", [64, NH, TS], BF16)
            rkb = P.sb("rkb", [64, NH, TS], BF16)
            mean, ex2, tmp = cwx, Em, Ex
            P.op("vector", lambda e, yt=yt: e.tensor_copy(yb[:], yt[:]), r=ykeys, w=["yb"])
            P.op("scalar", lambda e, yt=yt: e.activation(yq[:], yt[:], AF.Square), r=ykeys, w=["yq"])
            P.op("gpsimd", lambda e, tl=tl: e.tensor_tensor(tmp[:], tl["r"][:], tl["k"][:], ALU.mult), r=[ik("r"), ik("k")], w=["Ex"])
            for h in range(NH):
                P.op("vector", lambda e, h=h: e.tensor_scalar(rkb[:, h, :], tmp[:, h, :], gpar[:, 2 * NH + h:2 * NH + h + 1], None, ALU.mult),
                     r=["Ex", "gpar"], w=["rkb%d" % h])
            b1 = P.ps("bank0", [128, 512])
            b2 = P.ps("bank1", [128, 512])
            b3 = P.ps("bank2", [128, 512])
            for hp in range(NH // 2):
                for hh in range(2):
                    h = hp * 2 + hh
                    csl = slice(hh * TS, (hh + 1) * TS)
                    P.op("tensor", lambda e, h=h, csl=csl: e.matmul(b1[0:64, csl], ones64[:], yb[:, h, :], start=True, stop=True),
                         r=["ones64", "yb"], w=["bank0"])
                    P.op("tensor", lambda e, h=h, csl=csl: e.matmul(b2[0:64, csl], ones64[:], yq[:, h, :], start=True, stop=True),
                         r=["ones64", "yq"], w=["bank1"])
                    P.op("tensor", lambda e, h=h, csl=csl: e.matmul(b3[0:64, csl], ones64[:], rkb[:, h, :], start=True, stop=True),
                         r=["ones64", "rkb%d" % h], w=["bank2"])
                hs = slice(hp * 2, hp * 2 + 2)
                P.op("vector", lambda e, hs=hs: e.tensor_scalar(mean[:, hs, :], b1[0:64, :].rearrange("p (h t) -> p h t", h=2), 1.0 / 64, None, ALU.mult),
                     r=["bank0"], w=["cwx"])
                P.op("vector", lambda e, hs=hs: e.tensor_scalar(ex2[:, hs, :], b2[0:64, :].rearrange("p (h t) -> p h t", h=2), 1.0 / 64, None, ALU.mult),
                     r=["bank1"], w=["Em"])
                P.op("scalar", lambda e, hs=hs: e.copy(tmp[:, hs, :], b3[0:64, :].rearrange("p (h t) -> p h t", h=2)), r=["bank2"], w=["Ex"])
            P.op("gpsimd", lambda e, tl=tl: e.tensor_tensor(tmp[:], tmp[:], tl["v"][:], ALU.mult), r=["Ex", ik("v")], w=["Ex"])
            msq = cw
            P.op("gpsimd", lambda e: e.tensor_tensor(msq[:], mean[:], mean[:], ALU.mult), r=["cwx"], w=[*cwk])
            P.op("vector", lambda e: e.tensor_tensor(ex2[:], ex2[:], msq[:], ALU.subtract), r=["Em", *cwk], w=["Em"])
            P.op("vector", lambda e: e.tensor_scalar(ex2[:], ex2[:], 64e-5, None, ALU.add), r=["Em"], w=["Em"])
            P.op("scalar", lambda e: e.activation(ex2[:], ex2[:], AF.Ln), r=["Em"], w=["Em"])
            P.op("scalar", lambda e: e.activation(ex2[:], ex2[:], AF.Exp, scale=-0.5), r=["Em"], w=["Em"])
            P.op("gpsimd", lambda e, yt=yt: e.tensor_tensor(msq[:], yt[:], mean[:], ALU.subtract), r=ykeys + ["cwx", *cwk], w=[*cwk])
            P.op("vector", lambda e: e.tensor_tensor(msq[:], msq[:], ex2[:], ALU.mult), r=[*cwk, "Em"], w=[*cwk])
            for h in range(NH):
                P.op("vector", lambda e, h=h: e.tensor_scalar(msq[:, h, :], msq[:, h, :], gpar[:, h:h + 1], gpar[:, NH + h:NH + h + 1], ALU.mult, ALU.add),
                     r=[*cwk, "gpar"], w=[*cwk])
            P.op("gpsimd", lambda e: e.tensor_tensor(msq[:], msq[:], tmp[:], ALU.add), r=[*cwk, "Ex"], w=[*cwk])
            P.op("vector", lambda e, yt=yt: e.tensor_tensor(yt[:], msq[:], gt[:], ALU.mult), r=[*cwk, "g_in"] + ykeys, w=ykeys)
        fin.append(P.dma("sync", yout[:, :, ti * TS:(ti + 1) * TS], yt[:], r=ykeys))
    P.finish(fin)
    P.emit()
    return nc


QB = 512
NEAR = (128, 0, -128, -256, -384)
DQK = 128
DV = 256


def rel_bucket_np(dist):
    n = np.maximum(dist, 0)
    nf = np.maximum(n, 1).astype(np.float32)
    large = 16 + (np.log(nf / np.float32(16)) / np.float32(math.log(128 / 16)) * np.float32(16)).astype(np.int32)
    large = np.minimum(large, 31)
    return np.where(n < 16, n, large)


def bias_tiles(rel_bias_h):
    p = np.arange(128)[:, None]
    j = np.arange(QB)[None, :]
    return np.stack([rel_bias_h[rel_bucket_np(dl + j - p)] for dl in NEAR]).astype(np.float32)


def build_attn(NHD, TQ, TK, qoff, lambda_init):
    nc, P = new_prog()
    ones = P.tiles["ones_bf"]
    qTd = P.dram("qT", [NHD, 2, 128, TQ], F32, "ExternalInput")
    kTd = P.dram("kT", [NHD, 2, 128, TK], F32, "ExternalInput")
    vd = P.dram("v", [NHD, TK, DV], F32, "ExternalInput")
    btd = P.dram("btile", [NHD, 5, 128, QB], F32, "ExternalInput")
    cfd = P.dram("cfar", [128, NHD], F32, "ExternalInput")
    lamd = P.dram("lam", [128, 4], F32, "ExternalInput")
    sgd = P.dram("subg", [128, 2], F32, "ExternalInput")
    od = P.dram("oT", [NHD * DV, TQ], F32, "ExternalOutput").rearrange("(h dv p) t -> h p dv t", p=128, dv=2)
    NKB = TK // 128
    qb16 = P.sb("qb16", [128, 2, TQ], BF16)
    kb16 = P.sb("kb16", [128, 2, TK], BF16)
    vb16 = P.sb("vb16", [128, NKB, DV], BF16)
    bt = P.sb("bt", [128, 5, QB], F32)
    cf = P.sb("cf", [128, NHD], F32)
    lam = P.sb("lam", [128, 4], F32)
    sg = P.sb("subg", [128, 2], F32)
    P.dma("sync", cf[:], cfd, w=["cf"])
    P.dma("sync", lam[:], lamd, w=["lam"])
    P.dma("sync", sg[:], sgd, w=["subg"])
    lp = P.sb("lamp", [128, 2], BF16)
    le = P.sb("lame", [128, 2], F32)
    nl = P.sb("neglam", [128, 1], F32)
    lps = P.ps("bank7", [128, 512])
    P.op("vector", lambda e: e.tensor_tensor(lp[:, 0:1], lam[:, 0:1], lam[:, 1:2], ALU.mult), r=["lam"], w=["lamp0"])
    P.op("vector", lambda e: e.tensor_tensor(lp[:, 1:2], lam[:, 2:3], lam[:, 3:4], ALU.mult), r=["lam"], w=["lamp1"])
    P.op("tensor", lambda e: e.matmul(lps[:, 0:2], ones[:], lp[:], start=True, stop=True), r=["ones", "lamp0", "lamp1"], w=["bank7"])
    P.op("scalar", lambda e: e.activation(le[:], lps[:, 0:2], AF.Exp), r=["bank7"], w=["lame"])
    P.op("vector", lambda e: e.tensor_tensor(nl[:], le[:, 1:2], le[:, 0:1], ALU.subtract), r=["lame"], w=["neglam"])
    P.op("vector", lambda e: e.tensor_scalar(nl[:], nl[:], -float(lambda_init), None, ALU.add), r=["neglam"], w=["neglam"])
    scale = DQK ** -0.5
    fin = []
    it = 0
    madd = P.sb("maskadd", [128, 5, QB], F32)
    P.op("gpsimd", lambda e: e.memset(madd[:], 0.0), w=["maskadd"])
    for ni, dl in enumerate(NEAR):
        if dl < 128:
            P.op("gpsimd", lambda e, ni=ni, dl=dl: e.affine_select(madd[:, ni, :], madd[:, ni, :], [[1, QB]], ALU.is_ge, -1e30, base=dl, channel_multiplier=-1),
                 r=["maskadd"], w=["maskadd"])
    for hd in range(NHD):
        P.dma("gpsimd", qb16[:], qTd[hd].rearrange("c p t -> p c t"), w=["qb16"])
        P.dma("gpsimd", kb16[:], kTd[hd].rearrange("c p t -> p c t"), w=["kb16"])
        vview = vd[hd].rearrange("(kb p) d -> p kb d", p=128)
        for k0_ in range(0, NKB, 16):
            k1_ = min(NKB, k0_ + 16)
            P.dma("gpsimd", vb16[:, k0_:k1_, :], vview[:, k0_:k1_, :], w=["vb16"])
        P.dma("sync", bt[:], btd[hd].rearrange("n p j -> p n j"), w=["bt"])
        P.op("gpsimd", lambda e: e.tensor_tensor(bt[:], bt[:], madd[:], ALU.add), r=["bt", "maskadd"], w=["bt"])
        DEPTH_S = 3
        SBANKS = (0, 1, 6, 7)
        its = []
        for qb in range(TQ // QB):
            for c in range(2):
                nkb = (qoff + qb * QB + QB) // 128
                for kb in range(nkb):
                    its.append((qb, c, kb, nkb))
        O0 = P.ps("bank2", [128, 512])
        O1 = P.ps("bank3", [128, 512])
        L = P.ps("bank4", [128, 512])
        on = P.sb("on", [128, 2, 2, QB], F32)
        of = P.sb("attn_o", [128, 2, QB], F32)
        oq = P.sb("attn_oq", [128, 2, QB], BF16)
        rl = P.sb("rl", [128, QB], F32)
        rs = P.sb("attn_rs", [128, QB], F32)
        ms = P.ps("bank5", [128, 512])

        def emit_s(i):
            qb, c, kb, nkb = its[i]
            q0 = qoff + qb * QB
            ql = slice(qb * QB, (qb + 1) * QB)
            k0 = kb * 128
            dl = q0 - k0
            sl = i % 4
            S = P.ps("bank%d" % SBANKS[sl], [128, 512])
            pt = P.sb("pt%d" % sl, [128, QB], BF16)
            kS, kpt = "bank%d" % SBANKS[sl], "pt%d" % sl
            P.op("tensor", lambda e: e.matmul(S[:], kb16[:, c, k0:k0 + 128], qb16[:, c, ql], start=True, stop=True),
                 r=["kb16", "qb16"], w=[kS])
            if dl >= 256:
                P.op("scalar", lambda e, hd=hd: e.activation(pt[:], S[:], AF.Exp, bias=cf[:, hd:hd + 1], scale=scale),
                     r=[kS, "cf"], w=[kpt])
            else:
                ni = NEAR.index(dl)
                sbuf = P.sb("snear%d" % sl, [128, QB], F32)
                ksn = "snear%d" % sl
                P.op("vector", lambda e: e.scalar_tensor_tensor(sbuf[:], S[:], scale, bt[:, ni, :], ALU.mult, ALU.add),
                     r=[kS, "bt"], w=[ksn])
                P.op("scalar", lambda e: e.activation(pt[:], sbuf[:], AF.Exp), r=[ksn], w=[kpt])

        def emit_pv(i):
            qb, c, kb, nkb = its[i]
            ql = slice(qb * QB, (qb + 1) * QB)
            sl = i % 4
            pt = P.tiles["pt%d" % sl]
            kpt = "pt%d" % sl
            st, sp = (kb == 0), (kb == nkb - 1)
            P.op("tensor", lambda e: e.matmul(O0[:], vb16[:, kb, 0:128], pt[:], start=st, stop=sp), r=["vb16", kpt], w=["bank2"])
            P.op("tensor", lambda e: e.matmul(O1[:], vb16[:, kb, 128:256], pt[:], start=st, stop=sp), r=["vb16", kpt], w=["bank3"])
            P.op("tensor", lambda e: e.matmul(L[:], ones[:], pt[:], start=st, stop=sp), r=["ones", kpt], w=["bank4"])
            if not sp:
                return
            P.op("vector", lambda e: e.reciprocal(rl[:], L[:]), r=["bank4"], w=["rl"])
            P.op("vector", lambda e: e.tensor_tensor(on[:, c, 0, :], O0[:], rl[:], ALU.mult), r=["bank2", "rl"], w=["on%d0" % c])
            P.op("vector", lambda e: e.tensor_tensor(on[:, c, 1, :], O1[:], rl[:], ALU.mult), r=["bank3", "rl"], w=["on%d1" % c])
            if c == 0:
                return
            for dv in range(2):
                P.op("vector", lambda e, dv=dv: e.scalar_tensor_tensor(of[:, dv, :], on[:, 1, dv, :], nl[:, 0:1], on[:, 0, dv, :], ALU.mult, ALU.add),
                     r=["on1%d" % dv, "on0%d" % dv, "neglam"], w=["attn_o%d" % dv])
            P.op("scalar", lambda e: e.activation(oq[:], of[:], AF.Square), r=["attn_o0", "attn_o1"], w=["attn_oq"])
            for dv in range(2):
                P.op("tensor", lambda e, dv=dv: e.matmul(ms[:], ones[:], oq[:, dv, :], start=(dv == 0), stop=(dv == 1)),
                     r=["ones", "attn_oq"], w=["bank5"])
            P.op("vector", lambda e: e.tensor_scalar(rs[:], ms[:], 1.0 / DV, LN_EPS, ALU.mult, ALU.add), r=["bank5"], w=["attn_rs"])
            P.op("scalar", lambda e: e.activation(rs[:], rs[:], AF.Ln), r=["attn_rs"], w=["attn_rs"])
            P.op("scalar", lambda e: e.activation(rs[:], rs[:], AF.Exp, scale=-0.5), r=["attn_rs"], w=["attn_rs"])
            for dv in range(2):
                P.op("vector", lambda e, dv=dv: e.tensor_tensor(of[:, dv, :], of[:, dv, :], rs[:], ALU.mult), r=["attn_o%d" % dv, "attn_rs"], w=["attn_o%d" % dv])
                P.op("gpsimd", lambda e, dv=dv: e.tensor_scalar(of[:, dv, :], of[:, dv, :], sg[:, dv:dv + 1], 1.0 - float(lambda_init), ALU.mult, ALU.mult),
                     r=["attn_o%d" % dv, "subg"], w=["attn_o%d" % dv])
            fin.append(P.dma("sync", od[hd][:, :, ql], of[:], r=["attn_o0", "attn_o1"]))

        for i in range(len(its) + DEPTH_S):
            if i < len(its):
                emit_s(i)
            if i - DEPTH_S >= 0:
                emit_pv(i - DEPTH_S)
    P.finish(fin)
    P.emit()
    return nc


def kernel(**inputs):
    inp = {k: np.asarray(v) for k, v in inputs.items()}
    x = inp["x"]
    B, T, _ = x.shape
    NQ = NCORES // B
    TPC = T // NQ
    NT = TPC // TT
    cores = [(c // NQ, c % NQ) for c in range(NCORES)]
    ids = list(range(NCORES))
    progs = {}

    def prog(key, fn):
        return fn()

    def run(nc, in_maps):
        return run_bass_kernel_spmd(nc, in_maps, core_ids=ids).results

    def shard_tok(aT, halo=0):
        outs = []
        for (b, q) in cores:
            t0 = q * TPC
            if halo == 0:
                outs.append(np.ascontiguousarray(aT[b, :, t0:t0 + TPC]))
            else:
                buf = np.zeros((aT.shape[1], halo + TPC), np.float32)
                buf[:, halo:] = aT[b, :, t0:t0 + TPC]
                if q > 0:
                    buf[:, :halo] = aT[b, :, t0 - halo:t0]
                outs.append(buf)
        return outs

    def gather_tok(res, name, C):
        full = np.empty((B, C, T), np.float32)
        for (b, q), r in zip(cores, res):
            full[b, :, q * TPC:(q + 1) * TPC] = r[name]
        return full

    xT = np.ascontiguousarray(np.transpose(x, (0, 2, 1)))
    v_first = None
    for i in range(DEPTH):
        j = i // 2
        if i % 2 == 0:
            vres = j > 0
            nc = prog(("pre", vres), lambda: build_pre(NT, vres))
            par = pre_params(inp["rw_mu"][j], inp["rw_w0"][j], inp["rw_a0"][j], inp["rw_v0"][max(j - 1, 0)],
                             inp["rw_k_k"][j], inp["rw_k_a"][j])
            xs = shard_tok(xT, 1)
            vfs = shard_tok(v_first) if vres else None
            maps = []
            for ci in range(NCORES):
                m = dict(x=xs[ci], wr=inp["rw_w_rkv"][j, 0], wk=inp["rw_w_rkv"][j, 1], wv=inp["rw_w_rkv"][j, 2],
                         w1=inp["rw_w1"][j], w2=inp["rw_w2"][j], a1=inp["rw_a1"][j], a2=inp["rw_a2"][j],
                         g1=inp["rw_g1"][j], g2=inp["rw_g2"][j], par=par)
                if vres:
                    m.update(v1=inp["rw_v1"][j - 1], v2=inp["rw_v2"][j - 1], vfirst=vfs[ci])
                maps.append(m)
            res = run(nc, maps)
            pre = {n: gather_tok(res, "o_" + n, D) for n in ("r", "k", "v", "a", "b", "lw", "g")}
            del res
            if not vres:
                v_first = pre["v"]
            NHC = 32 // NQ
            nc = prog("scan", lambda: build_scan(NHC, T, post=True))
            maps = []
            for (b, hg) in cores:
                rows = slice(hg * NHC * 64, (hg + 1) * NHC * 64)
                m = {n: np.ascontiguousarray(pre[n][b, rows, :]) for n in ("r", "k", "v", "a", "b", "lw", "g")}
                hs = slice(hg * NHC, (hg + 1) * NHC)
                gp = [inp["rw_lnx_g"][j].reshape(32, 64)[hs].T, inp["rw_lnx_b"][j].reshape(32, 64)[hs].T, inp["rw_r_k"][j][hs].T]
                m["gpar"] = np.ascontiguousarray(np.concatenate(gp, axis=1).astype(np.float32))
                maps.append(m)
            del pre
            res = run(nc, maps)
            yT = np.empty((B, D, T), np.float32)
            for (b, hg), r in zip(cores, res):
                yT[b, hg * NHC * 64:(hg + 1) * NHC * 64, :] = r["y"]
            del res
            w_o = inp["rw_w_o"][j]
        else:
            lambda_init = 0.8 - 0.6 * math.exp(-0.3 * i)
            nc = prog("qkv", lambda: build_proj(NT, 3 * D, "plain"))
            xs = shard_tok(xT)
            res = run(nc, [dict(x=xs[ci], w=inp["da_w_qkv"][j]) for ci in range(NCORES)])
            qkvT = gather_tok(res, "out", 3 * D)
            del res
            NHD = 8 // NQ
            nc = prog(("attn", i), lambda: build_attn(NHD, T, T, 0, lambda_init))
            maps = []
            for (b, hp) in cores:
                hl = [hp * NHD + hd for hd in range(NHD)]
                qT = np.stack([np.stack([qkvT[b, h * 256 + c2 * 128:h * 256 + (c2 + 1) * 128, :] for c2 in range(2)]) for h in hl])
                kT = np.stack([np.stack([qkvT[b, D + h * 256 + c2 * 128:D + h * 256 + (c2 + 1) * 128, :] for c2 in range(2)]) for h in hl])
                vv = np.stack([np.ascontiguousarray(qkvT[b, 2 * D + h * 256:2 * D + (h + 1) * 256, :].T) for h in hl])
                maps.append(dict(qT=np.ascontiguousarray(qT), kT=np.ascontiguousarray(kT), v=vv,
                                 btile=np.stack([bias_tiles(inp["rel_bias"][:, h]) for h in hl]),
                                 cfar=np.ascontiguousarray(np.broadcast_to(inp["rel_bias"][31, hl][None, :], (128, NHD)).astype(np.float32)),
                                 lam=np.ascontiguousarray(inp["da_lam"][j].T), subg=np.ascontiguousarray(inp["da_subln_g"][j].reshape(2, 128).T)))
            del qkvT
            res = run(nc, maps)
            yT = np.empty((B, D, T), np.float32)
            for (b, hp), r in zip(cores, res):
                yT[b, hp * NHD * 256:(hp + 1) * NHD * 256, :] = r["oT"]
            del res
            w_o = inp["da_w_o"][j]
        nc = prog("resln", lambda: build_proj(NT, D, "resln"))
        ys = shard_tok(yT)
        xs = shard_tok(xT)
        par = np.ascontiguousarray(np.concatenate([chunkvec(inp["ln_g"][i, 0]), chunkvec(inp["ln_b"][i, 0])], axis=1))
        res = run(nc, [dict(x=ys[ci], res=xs[ci], w=w_o, par=par) for ci in range(NCORES)])
        x1T = gather_tok(res, "out", D)
        del res, ys, xs, yT
        nc = prog("ffn", lambda: build_ffn(NT))
        xs = shard_tok(x1T, 2)
        par = ffn_params(inp["ff_conv_w"][i], inp["ff_conv_b"][i], inp["ln_g"][i, 1], inp["ln_b"][i, 1])
        res = run(nc, [dict(x=xs[ci], wup=inp["ff_w_up"][i], wdn=inp["ff_w_down"][i], par=par) for ci in range(NCORES)])
        xT = gather_tok(res, "out", D)
        del res, xs, x1T
    return np.ascontiguousarray(np.transpose(xT, (0, 2, 1))).astype(np.float32)
```
